# Optimizing a Trainium2 kernel written in Bass

```python
import math
import jax, jax.numpy as jnp
from jax import lax
import numpy as np

D_MODEL = 1024
BATCH = 4
SEQ = 8192
DEPTH = 1

N_Q_HEADS = 8
N_KV_HEADS = 2
HEAD_DIM = 64
WINDOW = 128
ATTN_BLOCK = WINDOW
ROPE_THETA = 10000.0
MLSTM_HEADS = 4
MLSTM_HEAD_DIM = 128
MLSTM_CHUNK = 64
CONV_WIDTH = 4
D_FF = -(-8 * D_MODEL // (3 * 256)) * 256
NORM_EPS = 1e-6

ATTN_Q_WIDTH = N_Q_HEADS * HEAD_DIM
ATTN_KV_WIDTH = N_KV_HEADS * HEAD_DIM
MLSTM_WIDTH = MLSTM_HEADS * MLSTM_HEAD_DIM
IN_SPLIT_SIZES = (ATTN_Q_WIDTH, ATTN_KV_WIDTH, ATTN_KV_WIDTH,
                  MLSTM_WIDTH, MLSTM_WIDTH, MLSTM_WIDTH, MLSTM_WIDTH,
                  MLSTM_HEADS, MLSTM_HEADS, D_MODEL, D_MODEL)
IN_WIDTH = sum(IN_SPLIT_SIZES)

kernel_name = 'hybrid_swa_sink_mlstm_gated_block'


def rms_norm(x, g):
    xf = x.astype(jnp.float32)
    y = xf * lax.rsqrt(jnp.mean(xf * xf, axis=-1, keepdims=True) + NORM_EPS)
    return (y * g.astype(jnp.float32)).astype(x.dtype)


def modulate(h, shift, scale):
    return h * (1 + scale[:, None, :]) + shift[:, None, :]


def rope(t, positions):
    half = HEAD_DIM // 2
    inv_freq = ROPE_THETA ** (-2.0 * jnp.arange(half, dtype=jnp.float32) / HEAD_DIM)
    ang = positions.astype(jnp.float32)[..., None] * inv_freq
    cos = jnp.cos(ang)[:, :, None, :]
    sin = jnp.sin(ang)[:, :, None, :]
    tf = t.astype(jnp.float32)
    t1, t2 = tf[..., :half], tf[..., half:]
    return jnp.concatenate([t1 * cos - t2 * sin, t2 * cos + t1 * sin], axis=-1).astype(t.dtype)


def causal_depthwise_conv(u, w, b):
    S = u.shape[1]
    up = jnp.pad(u, ((0, 0), (CONV_WIDTH - 1, 0), (0, 0)))
    out = b
    for j in range(CONV_WIDTH):
        out = out + up[:, j:j + S] * w[j]
    return out


def sliding_window_attention(q, k, v, sinks):
    B, S = q.shape[0], q.shape[1]
    nb = S // ATTN_BLOCK
    g = N_Q_HEADS // N_KV_HEADS
    qb = q.astype(jnp.float32).reshape(B, nb, ATTN_BLOCK, N_KV_HEADS, g, HEAD_DIM) * (HEAD_DIM ** -0.5)

    def band(t):
        t = t.astype(jnp.float32).reshape(B, nb, ATTN_BLOCK, N_KV_HEADS, HEAD_DIM)
        prev = jnp.pad(t, ((0, 0), (1, 0), (0, 0), (0, 0), (0, 0)))[:, :-1]
        return jnp.concatenate([prev, t], axis=2)

    kb, vb = band(k), band(v)
    s = jnp.einsum('bnqhgd,bnkhd->bnhgqk', qb, kb)
    qi = jnp.arange(ATTN_BLOCK)[:, None]
    kj = jnp.arange(2 * ATTN_BLOCK)[None, :]
    rel = kj - ATTN_BLOCK
    in_band = (rel <= qi) & (qi - rel < WINDOW)
    blk = jnp.arange(nb)[:, None, None]
    mask = in_band[None] & ((blk > 0) | (kj[None] >= ATTN_BLOCK))
    s = jnp.where(mask[None, :, None, None], s, -jnp.inf)
    sink = sinks.astype(jnp.float32).reshape(N_KV_HEADS, g)[None, None, :, :, None, None]
    m = jnp.maximum(jnp.max(s, axis=-1, keepdims=True), sink)
    p = jnp.exp(s - m)
    p = p / (jnp.sum(p, axis=-1, keepdims=True) + jnp.exp(sink - m))
    o = jnp.einsum('bnhgqk,bnkhd->bnqhgd', p, vb)
    return o.reshape(B, S, N_Q_HEADS * HEAD_DIM)


def mlstm_chunkwise(q, k, v, i_pre, f_pre):
    B, S, H, D = q.shape
    L = MLSTM_CHUNK
    nc = S // L

    def chunks(t):
        return t.astype(jnp.float32).reshape(B, nc, L, H, D).transpose(0, 3, 1, 2, 4)

    qc = chunks(q)
    kc = chunks(k) * (D ** -0.5)
    vc = chunks(v)
    ig = i_pre.astype(jnp.float32).reshape(B, nc, L, H).transpose(0, 3, 1, 2)
    logf = jax.nn.log_sigmoid(f_pre.astype(jnp.float32)).reshape(B, nc, L, H).transpose(0, 3, 1, 2)
    b = jnp.cumsum(logf, axis=-1)
    b_last = b[..., -1]
    a = b_last[..., None] - b + ig

    def step(carry, inp):
        C, n, m = carry
        k_c, v_c, a_c, bl_c = inp
        m_new = jnp.maximum(bl_c + m, jnp.max(a_c, axis=-1))
        decay = jnp.exp(bl_c + m - m_new)
        kw = k_c * jnp.exp(a_c - m_new[..., None])[..., None]
        C_new = decay[..., None, None] * C + jnp.einsum('bhld,bhle->bhde', kw, v_c)
        n_new = decay[..., None] * n + jnp.sum(kw, axis=-2)
        return (C_new, n_new, m_new), (C, n, m)

    init = (jnp.zeros((B, H, D, D), jnp.float32), jnp.zeros((B, H, D), jnp.float32),
            jnp.zeros((B, H), jnp.float32))
    xs = (kc.transpose(2, 0, 1, 3, 4), vc.transpose(2, 0, 1, 3, 4),
          a.transpose(2, 0, 1, 3), b_last.transpose(2, 0, 1))
    _, (C_prev, n_prev, m_prev) = lax.scan(step, init, xs)
    C_prev = C_prev.transpose(1, 2, 0, 3, 4)
    n_prev = n_prev.transpose(1, 2, 0, 3)
    m_prev = m_prev.transpose(1, 2, 0)

    causal = jnp.tril(jnp.ones((L, L), dtype=bool))
    d_mat = jnp.where(causal, b[..., :, None] - b[..., None, :] + ig[..., None, :], -jnp.inf)
    inter = b + m_prev[..., None]
    m_t = jnp.maximum(inter, jnp.max(d_mat, axis=-1))
    w_intra = jnp.exp(d_mat - m_t[..., None])
    w_inter = jnp.exp(inter - m_t)
    scores = jnp.einsum('bhctd,bhcsd->bhcts', qc, kc) * w_intra
    num = jnp.einsum('bhcts,bhcsd->bhctd', scores, vc) + \
        w_inter[..., None] * jnp.einsum('bhctd,bhcde->bhcte', qc, C_prev)
    den = jnp.sum(scores, axis=-1) + w_inter * jnp.einsum('bhctd,bhcd->bhct', qc, n_prev)
    h = num / jnp.maximum(jnp.abs(den), jnp.exp(-m_t))[..., None]
    return h.transpose(0, 2, 3, 1, 4).reshape(B, S, H, D)


def head_layer_norm(h, w):
    mu = jnp.mean(h, axis=-1, keepdims=True)
    var = jnp.mean(jnp.square(h - mu), axis=-1, keepdims=True)
    y = (h - mu) * lax.rsqrt(var + NORM_EPS)
    return y.reshape(h.shape[0], h.shape[1], -1) * w.astype(jnp.float32)


def setup_inputs(seed: int = 0) -> dict:
    key = jax.random.key(seed)
    ks = jax.random.split(key, 24)
    f32 = jnp.float32

    def dense(k, shape, fan_in, scale=1.0):
        return jax.random.normal(k, shape, f32) * (scale * fan_in ** -0.5)

    def gain(k, width):
        return 1.0 + 0.05 * jax.random.normal(k, (DEPTH, width), f32)

    x = jax.random.normal(ks[0], (BATCH, SEQ, D_MODEL), f32)
    c = jax.random.normal(ks[1], (BATCH, D_MODEL), f32)
    offsets = jax.random.randint(ks[2], (BATCH, 1), 0, 4096, dtype=jnp.int32)
    positions = (offsets + jnp.arange(SEQ, dtype=jnp.int32)[None, :]).astype(jnp.int32)
    w_ada = dense(ks[3], (DEPTH, D_MODEL, 6 * D_MODEL), D_MODEL, 0.5)
    b_ada = 0.02 * jax.random.normal(ks[4], (DEPTH, 6 * D_MODEL), f32)
    g_pre_mix = gain(ks[5], D_MODEL)
    g_post_mix = gain(ks[6], D_MODEL)
    w_in = dense(ks[7], (DEPTH, D_MODEL, IN_WIDTH), D_MODEL)
    b_i = 0.1 * jax.random.normal(ks[8], (DEPTH, MLSTM_HEADS), f32)
    b_f = jnp.linspace(3.0, 6.0, MLSTM_HEADS, dtype=f32)[None, :] + \
        0.1 * jax.random.normal(ks[9], (DEPTH, MLSTM_HEADS), f32)
    b_if = jnp.concatenate([b_i, b_f], axis=-1)
    conv_w = dense(ks[10], (DEPTH, CONV_WIDTH, 2 * MLSTM_WIDTH), CONV_WIDTH)
    conv_b = 0.02 * jax.random.normal(ks[11], (DEPTH, 2 * MLSTM_WIDTH), f32)
    attn_sinks = 0.5 * jax.random.normal(ks[12], (DEPTH, N_Q_HEADS), f32)
    mlstm_norm_w = gain(ks[13], MLSTM_WIDTH)
    w_branch_attn = dense(ks[14], (DEPTH, ATTN_Q_WIDTH, D_MODEL), ATTN_Q_WIDTH)
    w_branch_mlstm = dense(ks[15], (DEPTH, MLSTM_WIDTH, D_MODEL), MLSTM_WIDTH)
    w_out = dense(ks[16], (DEPTH, D_MODEL, D_MODEL), D_MODEL)
    g_pre_ffn = gain(ks[17], D_MODEL)
    g_post_ffn = gain(ks[18], D_MODEL)
    w_ffn_gate = dense(ks[19], (DEPTH, D_MODEL, D_FF), D_MODEL)
    w_ffn_up = dense(ks[20], (DEPTH, D_MODEL, D_FF), D_MODEL)
    w_ffn_down = dense(ks[21], (DEPTH, D_FF, D_MODEL), D_FF)
    return {'x': x, 'c': c, 'positions': positions, 'w_ada': w_ada, 'b_ada': b_ada,
            'g_pre_mix': g_pre_mix, 'g_post_mix': g_post_mix, 'w_in': w_in, 'b_if': b_if,
            'conv_w': conv_w, 'conv_b': conv_b, 'attn_sinks': attn_sinks,
            'mlstm_norm_w': mlstm_norm_w, 'w_branch_attn': w_branch_attn,
            'w_branch_mlstm': w_branch_mlstm, 'w_out': w_out, 'g_pre_ffn': g_pre_ffn,
            'g_post_ffn': g_post_ffn, 'w_ffn_gate': w_ffn_gate, 'w_ffn_up': w_ffn_up,
            'w_ffn_down': w_ffn_down}


def reference(x, c, positions, w_ada, b_ada, g_pre_mix, g_post_mix, w_in, b_if, conv_w, conv_b,
              attn_sinks, mlstm_norm_w, w_branch_attn, w_branch_mlstm, w_out, g_pre_ffn,
              g_post_ffn, w_ffn_gate, w_ffn_up, w_ffn_down):
    B, S, _ = x.shape
    split_points = np.cumsum(IN_SPLIT_SIZES)[:-1].tolist()
    for l in range(DEPTH):
        mod = c @ w_ada[l] + b_ada[l]
        shift_m, scale_m, gate_m, shift_f, scale_f, gate_f = jnp.split(mod, 6, axis=-1)

        h = modulate(rms_norm(x, g_pre_mix[l]), shift_m, scale_m)
        proj = h @ w_in[l]
        q_a, k_a, v_a, q_m, k_m, v_m, o_m, i_m, f_m, g_a, g_m = jnp.split(proj, split_points, axis=-1)

        q_a = rope(q_a.reshape(B, S, N_Q_HEADS, HEAD_DIM), positions)
        k_a = rope(k_a.reshape(B, S, N_KV_HEADS, HEAD_DIM), positions)
        v_a = v_a.reshape(B, S, N_KV_HEADS, HEAD_DIM)
        y_a = sliding_window_attention(q_a, k_a, v_a, attn_sinks[l]).astype(x.dtype)

        qk_m = jax.nn.silu(causal_depthwise_conv(jnp.concatenate([q_m, k_m], axis=-1), conv_w[l], conv_b[l]))
        q_m, k_m = jnp.split(qk_m, 2, axis=-1)
        i_pre = i_m + b_if[l][:MLSTM_HEADS]
        f_pre = f_m + b_if[l][MLSTM_HEADS:]
        h_m = mlstm_chunkwise(q_m.reshape(B, S, MLSTM_HEADS, MLSTM_HEAD_DIM),
                              k_m.reshape(B, S, MLSTM_HEADS, MLSTM_HEAD_DIM),
                              v_m.reshape(B, S, MLSTM_HEADS, MLSTM_HEAD_DIM), i_pre, f_pre)
        y_m = (jax.nn.sigmoid(o_m.astype(jnp.float32)) * head_layer_norm(h_m, mlstm_norm_w[l])).astype(x.dtype)

        merged = jax.nn.sigmoid(g_a) * (y_a @ w_branch_attn[l]) + jax.nn.sigmoid(g_m) * (y_m @ w_branch_mlstm[l])
        mix = merged @ w_out[l]
        x = x + gate_m[:, None, :] * rms_norm(mix, g_post_mix[l])

        h2 = modulate(rms_norm(x, g_pre_ffn[l]), shift_f, scale_f)
        ff = (jax.nn.silu(h2 @ w_ffn_gate[l]) * (h2 @ w_ffn_up[l])) @ w_ffn_down[l]
        x = x + gate_f[:, None, :] * rms_norm(ff, g_post_ffn[l])
    return x
```

```python
import numpy as np
import concourse.bass as bass
import concourse.mybir as mybir
from concourse.bass_utils import run_bass_kernel_spmd
from contextlib import ExitStack

F32 = mybir.dt.float32
BF16 = mybir.dt.bfloat16
I32 = mybir.dt.int32
AF = mybir.ActivationFunctionType
ALU = mybir.AluOpType
AX = mybir.AxisListType

D = 1024
KC = 8
DFF = 2816
NSLAB = DFF // 128
INW = 4872
EPS = 1e-6
T = 128
NB = T // 128
T2 = 256
RTW = 256
NB2 = T2 // 128
C_QA, C_KA, C_QM, C_KM, C_GA, C_GM, C_VM, C_OM, C_VA, C_IF = 0, 512, 640, 1152, 1664, 2688, 3712, 4224, 4736, 4864


class Buf:
    __slots__ = ("name", "w", "r", "const")

    def __init__(self, name=""):
        self.name = name
        self.w = None
        self.r = []
        self.const = False


class DmaSem:
    def __init__(self, h):
        self.h = h
        self.count = 0


class Sched:
    def __init__(self, nc, es):
        self.nc = nc
        self.es = es
        self.q = {e: [] for e in ("pe", "act", "dve", "pool", "sp")}
        self.sem = {e: es.enter_context(nc.semaphore("c_" + e)) for e in ("pe", "act", "dve", "pool")}
        self.cnt = {e: 0 for e in self.sem}
        self.seen = {e: {} for e in self.q}
        self.nsem = 0
        self.dsems = []

    def dsem(self):
        self.nsem += 1
        s = DmaSem(self.es.enter_context(self.nc.semaphore("d%d" % self.nsem)))
        self.dsems.append(s)
        return s

    def _deps(self, eng, reads, writes):
        evs = []
        for b in reads:
            if b.w is not None:
                evs.append(b.w)
        for b in writes:
            if b.w is not None:
                evs.append(b.w)
            evs.extend(b.r)
        need = {}
        seen = self.seen[eng]
        for (s, v, src) in evs:
            if src == "pe" and eng == "pe":
                continue
            k = s.num
            if seen.get(k, 0) >= v:
                continue
            if k not in need or need[k][1] < v:
                need[k] = (s, v)
        for k, (s, v) in need.items():
            seen[k] = v
        return list(need.values())

    def _mark(self, ev, reads, writes):
        for b in reads:
            if not b.const:
                b.r.append(ev)
        for b in writes:
            b.w = ev
            b.r = []

    def op(self, eng, fn, reads=(), writes=()):
        waits = self._deps(eng, reads, writes)
        self.cnt[eng] += 1
        sem = self.sem[eng]
        ev = (sem, self.cnt[eng], eng)
        self._mark(ev, reads, writes)

        def run(e):
            for s, v in waits:
                e.wait_ge(s, v)
            fn(e).then_inc(sem, 1)

        self.q[eng].append(run)

    def dma(self, queue, out, in_, reads=(), writes=(), sem=None):
        waits = self._deps(queue, reads, writes)
        if sem is None:
            sem = self.dsem()
        sem.count += 16
        ev = (sem.h, sem.count, "dma")
        self._mark(ev, reads, writes)
        h = sem.h

        def run(e):
            for s, v in waits:
                e.wait_ge(s, v)
            e.dma_start(out=out, in_=in_).then_inc(h, 16)

        self.q[queue].append(run)
        return sem

    def barrier(self):
        targets = [(self.sem[e], self.cnt[e]) for e in self.sem if self.cnt[e] > 0]
        targets += [(d.h, d.count) for d in self.dsems if d.count > 0]
        for eng in ("pe", "act", "dve", "sp"):
            seen = self.seen[eng]
            ws = []
            for s, v in targets:
                if seen.get(s.num, 0) < v:
                    seen[s.num] = v
                    ws.append((s, v))

            def run(e, ws=ws):
                for s, v in ws:
                    e.wait_ge(s, v)

            self.q[eng].append(run)

    def emit(self):
        nc = self.nc
        q = self.q
        with nc.Block() as block:
            @block.tensor
            def _(e):
                for f in q["pe"]:
                    f(e)

            @block.scalar
            def _(e):
                for f in q["act"]:
                    f(e)

            @block.vector
            def _(e):
                for f in q["dve"]:
                    f(e)

            @block.gpsimd
            def _(e):
                for f in q["pool"]:
                    f(e)

            @block.sync
            def _(e):
                for f in q["sp"]:
                    f(e)


class TT:
    def __init__(self, t, name):
        self.t = t
        self.b = Buf(name)
        self.ds = None

    def __getitem__(self, k):
        return self.t[k]


class _Stop(Exception):
    pass


def build(NTOK, NPRE, STOP=None):
    NT_MAIN = NTOK // T
    NT_PRE = NPRE // T
    nc = bass.Bass("TRN2", target_bir_lowering=False)

    def din(name, shape, dt=F32):
        return nc.dram_tensor(name, list(shape), dt, kind="ExternalInput").ap()

    xm = din("xm", [NTOK, D])
    xp = din("xp", [NPRE, D])
    posm = din("posm", [1, NTOK], I32)
    posp = din("posp", [1, T], I32)
    flag_d = din("flag", [128, 1])
    cT_d = din("cT", [128, KC])
    w_ada_d = din("w_ada", [128, 12, KC * 512])
    b_ada_d = din("b_ada", [1, 6 * D])
    gpm_d = din("gpm", [128, KC])
    gpf_d = din("gpf", [128, KC])
    gpostm_d = din("gpostm", [1, D])
    gpostf_d = din("gpostf", [1, D])
    w_in_d = din("w_in", [128, KC * INW])
    bif_d = din("bif", [1, 8])
    convw_d = din("convw", [128, 8 * 4])
    convb_d = din("convb", [128, 8])
    sinks_d = din("sinks", [1, 8])
    nw_d = din("nw", [1, 512])
    wa_d = din("wa", [128, 4 * D])
    wb_d = din("wb", [128, 4 * D])
    wo_d = din("wo", [128, KC * D])
    wg_d = din("wg", [128, KC * DFF])
    wu_d = din("wu", [128, KC * DFF])
    wd_d = din("wd", [128, NSLAB * D])
    cst_d = din("cst", [128, 4 * 128 + 1])
    out_d = nc.dram_tensor("out", [NTOK, D], F32, kind="ExternalOutput").ap()

    try:
      with ExitStack() as es:
        S = Sched(nc, es)
        cnt = [0]

        def stop(tag):
            if STOP == tag:
                S.barrier()
                S.emit()
                raise _Stop()

        PA = {}

        def sb(shape, dt, name=None, stack=None):
            if name in PA:
                return PA[name]
            cnt[0] += 1
            name = "s_" + (name or ("t%d" % cnt[0]))
            return TT((stack or es).enter_context(nc.sbuf_tensor(name, list(shape), dt)), name)

        def dsem_of(tt_):
            if tt_.ds is None:
                tt_.ds = S.dsem()
            return tt_.ds

        banks = [TT(es.enter_context(nc.psum_tensor("bank%d" % i, [128, 512], F32)), "bank%d" % i) for i in range(8)]
        rr = {"tr": [0, [0, 1]], "fm": [0, [2, 3]], "tm": [0, [4, 5]], "at": [0, [6, 7]]}

        def bank(pool):
            st = rr[pool]
            b = banks[st[1][st[0] % len(st[1])]]
            st[0] += 1
            return b

        def act(out, in_, func, reads, writes, scale=None, bias=None):
            kw = {}
            if scale is not None:
                kw["scale"] = scale
            if bias is not None:
                kw["bias"] = bias
            S.op("act", lambda e: e.activation(out=out, in_=in_, func=func, **kw), reads, writes)

        def ts(out, in0, s1, s2, op0, op1, reads, writes):
            if s2 is None:
                S.op("dve", lambda e: e.tensor_scalar(out=out, in0=in0, scalar1=s1, scalar2=None, op0=op0), reads, writes)
            else:
                S.op("dve", lambda e: e.tensor_scalar(out=out, in0=in0, scalar1=s1, scalar2=s2, op0=op0, op1=op1), reads, writes)

        def stt(out, in0, scalar, in1, op0, op1, reads, writes):
            S.op("dve", lambda e: e.scalar_tensor_tensor(out=out, in0=in0, scalar=scalar, in1=in1, op0=op0, op1=op1), reads, writes)

        def tt(out, in0, in1, op, reads, writes):
            S.op("dve", lambda e: e.tensor_tensor(out=out, in0=in0, in1=in1, op=op), reads, writes)

        def cp(eng, out, in_, reads, writes):
            if eng == "act":
                act(out, in_, AF.Copy, reads, writes)
            else:
                S.op("dve", lambda e: e.tensor_copy(out=out, in_=in_), reads, writes)

        def mm(items, reads, writes):
            def fn(e):
                ins = None
                for (o, l, r, st, sp) in items:
                    ins = e.matmul(o, lhsT=l, rhs=r, start=st, stop=sp)
                return ins
            S.op("pe", fn, reads, writes)

        def transposes(items, reads, writes):
            def fn(e):
                ins = None
                for (o, i, idn) in items:
                    ins = e.transpose(out=o, in_=i, identity=idn)
                return ins
            S.op("pe", fn, reads, writes)

        def rsqrt_inplace(v, n):
            act(v[:, 0:n], v[:, 0:n], AF.Sqrt, [v.b], [v.b])
            S.op("dve", lambda e: e.reciprocal(out=v[:, 0:n], in_=v[:, 0:n]), [v.b], [v.b])

        for nm_, sh_, dt_ in (("cst32", [128, 513], F32), ("ident", [128, 128], BF16), ("maskC", [128, 128], BF16),
                              ("maskP", [128, 128], BF16), ("rot", [128, 128], BF16), ("ones32", [128, 128], F32), ("ZR", [128, 144], BF16),
                              ("flag", [128, 1], F32), ("gpm", [128, KC], F32), ("gpf", [128, KC], F32),
                              ("convw", [128, 32], F32), ("convb", [128, 8], F32), ("cT", [128, KC], F32),
                              ("bif", [128, 8], F32), ("esink", [128, 8], F32), ("nwh", [128, 512], F32),
                              ("cbh", [128, 8], F32), ("cTb", [128, KC], BF16), ("DG", [128, 32, 128], BF16),
                              ("stg0", [128, 512], F32), ("stg1", [128, 512], F32), ("stg2", [128, 512], F32), ("stg3", [128, 512], F32), ("modT", [128, 32], F32),
                              ("a_m", [128, KC], F32), ("a_f", [128, KC], F32), ("gg_m", [128, D], F32),
                              ("gg_f", [128, D], F32)):
            t_ = sb(sh_, dt_, nm_)
            PA[nm_] = t_
        pw = es.enter_context(ExitStack())
        WIN = sb([128, KC * INW], BF16, "WIN", pw)
        WA = sb([128, 4 * D], BF16, "WA", pw)
        WB = sb([128, 4 * D], BF16, "WB", pw)
        WO = sb([128, KC * D], BF16, "WO", pw)
        p0 = es.enter_context(ExitStack())
        cst32 = sb([128, 4 * 128 + 1], F32, "cst32")
        S.dma("sp", cst32[:], cst_d, writes=[cst32.b], sem=dsem_of(cst32))
        ident = sb([128, 128], BF16, "ident")
        maskC = sb([128, 128], BF16, "maskC")
        maskP = sb([128, 128], BF16, "maskP")
        rot = sb([128, 128], BF16, "rot")
        for i, tcon in enumerate((ident, maskC, maskP, rot)):
            cp("dve", tcon[:], cst32[:, i * 128:(i + 1) * 128], [cst32.b], [tcon.b])
            tcon.b.const = True
        identf = cst32[:, 0:128]
        ucs = cst32[:, 128:256]
        invf = cst32[:, 512:513]
        ones32 = sb([128, 128], F32, "ones32")
        S.op("dve", lambda e: e.memset(ones32[:], 1.0), [], [ones32.b])
        ones32.b.const = True
        ZR = sb([128, 144], BF16, "ZR")
        S.op("dve", lambda e: e.memset(ZR[:], 0.0), [], [ZR.b])
        ZR.b.const = True
        flag = sb([128, 1], F32, "flag")
        S.dma("sp", flag[:], flag_d, writes=[flag.b], sem=dsem_of(flag))

        small = {}
        for name, src, shape in (("gpm", gpm_d, [128, KC]), ("gpf", gpf_d, [128, KC]), ("convw", convw_d, [128, 32]),
                                 ("convb", convb_d, [128, 8]), ("cT", cT_d, [128, KC])):
            small[name] = sb(shape, F32, name)
            S.dma("sp", small[name][:], src, writes=[small[name].b], sem=dsem_of(small[name]))
        bif = sb([128, 8], F32, "bif")
        S.dma("sp", bif[:], bif_d.partition_broadcast(128), writes=[bif.b], sem=dsem_of(bif))
        esink = sb([128, 8], F32, "esink")
        S.dma("sp", esink[:], sinks_d.partition_broadcast(128), writes=[esink.b], sem=dsem_of(esink))
        act(esink[:], esink[:], AF.Exp, [esink.b], [esink.b])
        nwh = sb([128, 512], F32, "nwh")
        S.dma("sp", nwh[:], nw_d.partition_broadcast(128), writes=[nwh.b], sem=dsem_of(nwh))
        ts(nwh[:], nwh[:], 0.5, None, ALU.mult, None, [nwh.b], [nwh.b])
        rows = sb([1, 2 * D], F32, "rows", p0)
        S.dma("sp", rows[0:1, 0:D], gpostm_d, writes=[rows.b], sem=dsem_of(rows))
        S.dma("sp", rows[0:1, D:2 * D], gpostf_d, writes=[rows.b], sem=dsem_of(rows))
        modrow = sb([1, 6 * D], F32, "modrow", p0)
        S.dma("sp", modrow[:], b_ada_d, writes=[modrow.b], sem=dsem_of(modrow))
        cbh = sb([128, 8], F32, "cbh")
        ts(cbh[:], small["convb"][:], 0.5, None, ALU.mult, None, [small["convb"].b], [cbh.b])
        cTb = sb([128, KC], BF16, "cTb")
        cp("dve", cTb[:], small["cT"][:], [small["cT"].b], [cTb.b])
        DG = sb([128, 32, 128], BF16, "DG")
        for i in range(32):
            ts(DG[:, i, :], identf, small["convw"][:, i:i + 1], None, ALU.mult, None, [cst32.b, small["convw"].b], [DG.b])
        DG.b.const = True

        stg = [sb([128, 512], F32, "stg%d" % i) for i in range(4)]
        stgi = [0]

        def load_cast(dst_flat, src_flat, n, wbuf):
            o = 0
            while o < n:
                c = min(512, n - o)
                st = stg[stgi[0] % 4]
                S.dma("sp", st[:, 0:c], src_flat[:, o:o + c], writes=[st.b], sem=dsem_of(st))
                cp("act" if stgi[0] % 2 == 0 else "dve", dst_flat[:, o:o + c], st[:, 0:c], [st.b], [wbuf])
                stgi[0] += 1
                o += c

        wada_b = sb([128, KC * 512], BF16, "wada_b", p0)
        for n in range(12):
            load_cast(wada_b[:], w_ada_d[:, n, :], KC * 512, wada_b.b)
            pb = bank("tm")
            mm([(pb[0:1, :], cTb[:, kc:kc + 1], wada_b[:, kc * 512:(kc + 1) * 512], kc == 0, kc == KC - 1) for kc in range(KC)],
               [cTb.b, wada_b.b], [pb.b])
            tt(modrow[0:1, n * 512:(n + 1) * 512], modrow[0:1, n * 512:(n + 1) * 512], pb[0:1, :], ALU.add,
               [modrow.b, pb.b], [modrow.b])
        modT = sb([128, 32], F32, "modT")
        pb = bank("tm")
        items = []
        for vi, v in enumerate((0, 1, 3, 4)):
            for kc in range(KC):
                items.append((pb[:, vi * 8 + kc:vi * 8 + kc + 1], modrow[0:1, v * D + kc * 128:v * D + (kc + 1) * 128],
                              ones32[0:1, 0:1], True, True))
        mm(items, [modrow.b, ones32.b], [pb.b])
        cp("dve", modT[:], pb[:, 0:32], [pb.b], [modT.b])
        amod = {}
        for nm, gsrc, sc_off, sh_off in (("m", small["gpm"], 8, 0), ("f", small["gpf"], 24, 16)):
            a = sb([128, KC], F32, "a_" + nm)
            stt(a[:], modT[:, sc_off:sc_off + 8], 1.0, gsrc[:], ALU.add, ALU.mult, [modT.b, gsrc.b], [a.b])
            amod[nm] = (a, sh_off)
        gg = {}
        for nm, goff, roff in (("m", 2 * D, 0), ("f", 5 * D, D)):
            tt(rows[0:1, roff:roff + D], rows[0:1, roff:roff + D], modrow[0:1, goff:goff + D], ALU.mult,
               [rows.b, modrow.b], [rows.b])
            g = sb([128, D], F32, "gg_" + nm)
            for c in range(2):
                pb = bank("tm")
                mm([(pb[:, :], ones32[0:1, :], rows[0:1, roff + c * 512:roff + (c + 1) * 512], True, True)],
                   [ones32.b, rows.b], [pb.b])
                cp("dve", g[:, c * 512:(c + 1) * 512], pb[:, :], [pb.b], [g.b])
            gg[nm] = g

        stop('ada')
        load_cast(WIN[:], w_in_d, KC * INW, WIN.b)
        load_cast(WA[:], wa_d, 4 * D, WA.b)
        load_cast(WB[:], wb_d, 4 * D, WB.b)
        load_cast(WO[:], wo_d, KC * D, WO.b)
        stop('wload')
        S.barrier()
        p0.close()
        p1 = es.enter_context(ExitStack())

        def win(kc, c0, c1):
            return WIN[:, kc * INW + c0:kc * INW + c1]

        XT = [sb([128, NB, D], F32, "XT%d" % i, p1) for i in range(2)]
        SQ = sb([128, D], BF16, "SQ", p1)
        XS = [sb([128, D], BF16, "XS%d" % i, p1) for i in range(2)]
        HTs = [sb([128, KC, T], BF16, "HT%d" % i, p1) for i in range(2)]
        stat = sb([128, 8], F32, "stat", p1)
        QTs = [sb([128, 4, T], BF16, "QT%d" % i, p1) for i in range(2)]
        QPRE = [sb([128, T], BF16, "QPRE%d" % i, p1) for i in range(2)]
        KT = [sb([128, 128 + T], BF16, "KT%d" % i, p1) for i in range(2)]
        COS = sb([128, RTW], F32, "COS", p1)
        SIN = sb([128, RTW], F32, "SIN", p1)
        RT1 = sb([128, RTW], F32, "RT1", p1)
        RT2 = sb([128, RTW], F32, "RT2", p1)
        PI_ = sb([128, RTW], I32, "PI_", p1)
        VA = [sb([128, 2, 66], BF16, "VA%d" % i, p1) for i in range(3)]
        UU = sb([128, 8, 4 + T], BF16, "UU", p1)
        CV = [sb([128, T], F32, "CV%d" % i, p1) for i in range(2)]
        TH = [sb([128, T], F32, "TH%d" % i, p1) for i in range(2)]
        QKMs = [sb([128, 8, T], BF16, "QKM%d" % i, p1) for i in range(2)]
        KU = [sb([128, 4, 128], BF16, "KU%d" % i, p1) for i in range(2)]
        VM = [sb([128, 4, 130], BF16, "VM%d" % i, p1) for i in range(2)]
        TO = [sb([128, 512], BF16, "TO%d" % i, p1) for i in range(2)]
        TG = sb([128, 2 * D], BF16, "TG", p1)
        MTOK = sb([128, D], BF16, "MTOK", p1)
        YA = sb([128, 512], BF16, "YA", p1)
        YM = sb([128, 512], BF16, "YM", p1)
        YAT = sb([128, 4, T], BF16, "YAT", p1)
        YMT = sb([128, 4, T], BF16, "YMT", p1)
        MT = sb([128, KC, T], BF16, "MT", p1)
        M1 = sb([128, 4 * T], F32, "M1", p1)
        M2 = sb([128, 4 * T], F32, "M2", p1)
        PT = [sb([128, 512], BF16, "PT%d" % i, p1) for i in range(2)]
        IFBs = [sb([128, NB, 8], F32, "IFB%d" % i, p1) for i in range(2)]
        GWs = [sb([128, NB, 24], F32, "GW%d" % i, p1) for i in range(2)]
        GL = [sb([128, 8], F32, "GL%d" % i, p1) for i in range(2)]
        DST = sb([128, 4, 129], F32, "DST", p1)
        CB = sb([128, 4, 130], BF16, "CB", p1)
        AT = [sb([128, 128], BF16, "AT%d" % i, p1) for i in range(1)]
        DEN = sb([128, 16], F32, "DEN", p1)
        CEN = sb([128, 4, 128], F32, "CEN", p1)
        TMPX = sb([128, D], F32, "TMPX", p1)
        OD = [Buf("od%d" % i) for i in range(NTOK // 128)]

        for v in VA:
            S.op("dve", lambda e, v=v: e.memset(v[:], 1.0), [], [v.b])
        for v in VM:
            S.op("dve", lambda e, v=v: e.memset(v[:], 1.0), [], [v.b])
        S.op("dve", lambda e: e.memset(DST[:], 0.0), [], [DST.b])
        S.op("dve", lambda e: e.memset(CB[:], 0.0), [], [CB.b])
        S.op("dve", lambda e: e.memset(UU[:], 0.0), [], [UU.b])
        S.op("dve", lambda e: e.memset(GL[1][:], 1.0), [], [GL[1].b])
        for k in KT:
            S.op("dve", lambda e, k=k: e.memset(k[:], 0.0), [], [k.b])

        cg = [0]

        def stage_A(xsrc_ap, xt, a, shoff, ht, nb, SQ, XS, stat, xdeps=()):
            S.dma("sp", xt[:, 0:nb, :], xsrc_ap, reads=list(xdeps), writes=[xt.b], sem=dsem_of(xt))
            for j in range(nb):
                act(SQ[:], xt[:, j, :], AF.Square, [xt.b], [SQ.b])
                S.op("dve", lambda e, j=j, stat=stat, SQ=SQ: e.reduce_sum(out=stat[:, j:j + 1], in_=SQ[:], axis=AX.X), [SQ.b], [stat.b])
            ts(stat[:, 0:nb], stat[:, 0:nb], 1.0 / D, EPS, ALU.mult, ALU.add, [stat.b], [stat.b])
            rsqrt_inplace(stat, nb)
            for j in range(nb):
                xs = XS[j % 2]
                act(xs[:], xt[:, j, :], AF.Copy, [xt.b, stat.b], [xs.b], scale=stat[:, j:j + 1])
                pb = bank("tr")
                pv = pb[:, :].bitcast(BF16)
                transposes([(pv[:, kc * 128:(kc + 1) * 128], xs[:, kc * 128:(kc + 1) * 128], ident[:]) for kc in range(KC)],
                           [xs.b, ident.b], [pb.b])
                for kc in range(KC):
                    dst = ht[:, kc, j * 128:(j + 1) * 128]
                    src = pv[:, kc * 128:(kc + 1) * 128]
                    if kc % 2 == 0:
                        act(dst, src, AF.Identity, [pb.b, a.b, modT.b], [ht.b], scale=a[:, kc:kc + 1],
                            bias=modT[:, shoff + kc:shoff + kc + 1])
                    else:
                        ts(dst, src, a[:, kc:kc + 1], modT[:, shoff + kc:shoff + kc + 1], ALU.mult, ALU.add,
                           [pb.b, a.b, modT.b], [ht.b])

        def fm_proj(c0):
            pb = bank("fm")
            mm([(pb[:, 0:T], win(kc, c0, c0 + 128), HTs[cg[0] % 2][:, kc, :], kc == 0, kc == KC - 1) for kc in range(KC)],
               [WIN.b, HTs[cg[0] % 2].b], [pb.b])
            return pb

        def rope_tables(pos_ap, w):
            S.dma("sp", PI_[:, 0:w], pos_ap.partition_broadcast(128), writes=[PI_.b], sem=dsem_of(PI_))
            cp("dve", RT1[:, 0:w], PI_[:, 0:w], [PI_.b], [RT1.b])
            ts(RT1[:, 0:w], RT1[:, 0:w], invf, None, ALU.mult, None, [RT1.b, cst32.b], [RT1.b])
            for tab, shift in ((SIN, 0.0), (COS, float(np.pi / 2))):
                ts(RT2[:, 0:w], RT1[:, 0:w], shift, None, ALU.add, None, [RT1.b], [RT2.b])
                ts(PI_[:, 0:w], RT2[:, 0:w], float(1 / (2 * np.pi)), None, ALU.mult, None, [RT2.b], [PI_.b])
                cp("dve", tab[:, 0:w], PI_[:, 0:w], [PI_.b], [tab.b])
                stt(RT2[:, 0:w], tab[:, 0:w], -6.28125, RT2[:, 0:w], ALU.mult, ALU.add, [tab.b, RT2.b], [RT2.b])
                stt(RT2[:, 0:w], tab[:, 0:w], -0.0019353071795864769, RT2[:, 0:w], ALU.mult, ALU.add, [tab.b, RT2.b], [RT2.b])
                ts(tab[:, 0:w], RT2[:, 0:w], float(np.pi), float(-2 * np.pi), ALU.is_gt, ALU.mult, [RT2.b], [tab.b])
                tt(RT2[:, 0:w], RT2[:, 0:w], tab[:, 0:w], ALU.add, [RT2.b, tab.b], [RT2.b])
                ts(tab[:, 0:w], RT2[:, 0:w], float(-np.pi), float(2 * np.pi), ALU.is_lt, ALU.mult, [RT2.b], [tab.b])
                tt(RT2[:, 0:w], RT2[:, 0:w], tab[:, 0:w], ALU.add, [RT2.b, tab.b], [RT2.b])
                act(tab[:, 0:w], RT2[:, 0:w], AF.Sin, [RT2.b], [tab.b])

        def rope_group(c0, dst_ap, dst_buf, qi, ro):
            pb = fm_proj(c0)
            qp = QPRE[qi % 2]
            cp("act", qp[:], pb[:, 0:T], [pb.b], [qp.b])
            pr = bank("fm")
            mm([(pr[:, 0:T], rot[:], qp[:], True, True)], [rot.b, qp.b], [pr.b])
            k_ = (qi % 4) * T
            tt(M1[:, k_:k_ + T], pr[:, 0:T], SIN[:, ro:ro + T], ALU.mult, [pr.b, SIN.b], [M1.b])
            tt(M2[:, k_:k_ + T], qp[:], COS[:, ro:ro + T], ALU.mult, [qp.b, COS.b], [M2.b])
            tt(dst_ap, M1[:, k_:k_ + T], M2[:, k_:k_ + T], ALU.add, [M1.b, M2.b], [dst_buf])

        def conv_group(g, c0):
            pb = fm_proj(c0)
            cp("dve", UU[:, g, 0:4], UU[:, g, T:T + 4], [UU.b], [UU.b])
            cp("act", UU[:, g, 4:4 + T], pb[:, 0:T], [pb.b], [UU.b])
            pc = bank("fm")
            mm([(pc[:, 0:T + 8], ident[:], ZR[:, 0:T + 8], True, False)]
               + [(pc[:, 3 - j:3 - j + T + 4], DG[:, g * 4 + j, :], UU[:, g, 0:T + 4], False, j == 3) for j in range(4)],
               [DG.b, UU.b, ident.b, ZR.b], [pc.b])
            cv = CV[g % 2]
            th = TH[g % 2]
            ts(cv[:], pc[:, 4:4 + T], small["convb"][:, g:g + 1], None, ALU.add, None, [pc.b, small["convb"].b], [cv.b])
            act(th[:], cv[:], AF.Tanh, [cv.b], [th.b], scale=0.5)
            stt(QKMs[cg[0] % 2][:, g, :], th[:], 1.0, cv[:], ALU.add, ALU.mult, [th.b, cv.b], [QKMs[cg[0] % 2].b])

        def tm_proj(j, full):
            blk = slice(j * 128, (j + 1) * 128)
            vm = VM[cg[0] % 2]
            pb = bank("tm")
            mm([(pb[:, :], HTs[cg[0] % 2][:, kc, blk], win(kc, C_VM, C_VM + 512), kc == 0, kc == KC - 1) for kc in range(KC)],
               [WIN.b, HTs[cg[0] % 2].b], [pb.b])
            cp("act", vm[:, :, 0:128], pb[:, :].rearrange("p (h d) -> p h d", h=4), [pb.b], [vm.b])
            if full:
                to = TO[cg[0] % 2]
                pb = bank("tm")
                mm([(pb[:, :], HTs[cg[0] % 2][:, kc, blk], win(kc, C_OM, C_OM + 512), kc == 0, kc == KC - 1) for kc in range(KC)],
                   [WIN.b, HTs[cg[0] % 2].b], [pb.b])
                act(to[:], pb[:, :], AF.Tanh, [pb.b], [to.b], scale=0.5)
            pb = bank("tm")
            mm([(pb[:, 0:136], HTs[cg[0] % 2][:, kc, blk], win(kc, C_VA, C_VA + 136), kc == 0, kc == KC - 1) for kc in range(KC)],
               [WIN.b, HTs[cg[0] % 2].b], [pb.b])
            va = VA[cg[0] % 3]
            cp("dve", va[:, :, 0:64], pb[:, 0:128].rearrange("p (h d) -> p h d", h=2), [pb.b], [va.b])
            S.op("dve", lambda e, va=va: e.memset(va[:, :, 64:65], 1.0), [], [va.b])
            tt(IFBs[cg[0] % 2][:, j, :], pb[:, 128:136], bif[:], ALU.add, [pb.b, bif.b], [IFBs[cg[0] % 2].b])

        def gate_math(nb):
            act(GWs[cg[0] % 2][:, 0:nb, 0:4], IFBs[cg[0] % 2][:, 0:nb, 4:8], AF.Exp, [IFBs[cg[0] % 2].b], [GWs[cg[0] % 2].b], scale=-1.0)
            ts(GWs[cg[0] % 2][:, 0:nb, 0:4], GWs[cg[0] % 2][:, 0:nb, 0:4], 1.0, None, ALU.add, None, [GWs[cg[0] % 2].b], [GWs[cg[0] % 2].b])
            act(GWs[cg[0] % 2][:, 0:nb, 0:4], GWs[cg[0] % 2][:, 0:nb, 0:4], AF.Ln, [GWs[cg[0] % 2].b], [GWs[cg[0] % 2].b])
            pb = bank("at")
            items = []
            for j in range(nb):
                items.append((pb[:, j * 8:j * 8 + 4], ucs, GWs[cg[0] % 2][:, j, 0:4], True, True))
                items.append((pb[:, j * 8 + 4:j * 8 + 8], ones32[:], GWs[cg[0] % 2][:, j, 0:4], True, True))
            mm(items, [cst32.b, ones32.b, GWs[cg[0] % 2].b], [pb.b])
            pv = pb[:, 0:nb * 8].rearrange("p (j c) -> p j c", c=8)
            tt(GWs[cg[0] % 2][:, 0:nb, 4:8], IFBs[cg[0] % 2][:, 0:nb, 0:4], pv[:, :, 0:4], ALU.add, [IFBs[cg[0] % 2].b, pb.b], [GWs[cg[0] % 2].b])
            act(GWs[cg[0] % 2][:, 0:nb, 20:24], GWs[cg[0] % 2][:, 0:nb, 4:8], AF.Exp, [GWs[cg[0] % 2].b], [GWs[cg[0] % 2].b])
            ts(GWs[cg[0] % 2][:, 0:nb, 8:12], GWs[cg[0] % 2][:, 0:nb, 20:24], float(0.25 / np.sqrt(128.0)), None, ALU.mult, None, [GWs[cg[0] % 2].b], [GWs[cg[0] % 2].b])
            ts(GWs[cg[0] % 2][:, 0:nb, 12:16], GWs[cg[0] % 2][:, 0:nb, 20:24], float(0.5 / np.sqrt(128.0)), None, ALU.mult, None, [GWs[cg[0] % 2].b], [GWs[cg[0] % 2].b])
            act(GWs[cg[0] % 2][:, 0:nb, 16:20], pv[:, :, 0:4], AF.Exp, [pb.b], [GWs[cg[0] % 2].b], scale=-1.0)
            return pb

        def mlstm_block(j, gpb, full):
            blk = slice(j * 128, (j + 1) * 128)
            par = cg[0] % 2
            vm = VM[par]
            ku = KU[par]
            glc, glp = GL[par], GL[1 - par]
            act(glc[:, 0:4], gpb[:, j * 8 + 4:j * 8 + 8], AF.Exp, [gpb.b], [glc.b], scale=-1.0)
            ts(glc[:, 4:8], glc[:, 0:4], 0.5, None, ALU.mult, None, [glc.b], [glc.b])
            pb = bank("tr")
            pv = pb[:, :].bitcast(BF16)
            transposes([(pv[:, h * 128:(h + 1) * 128], QKMs[cg[0] % 2][:, 4 + h, blk], ident[:]) for h in range(4)],
                       [QKMs[cg[0] % 2].b, ident.b], [pb.b])
            for h in range(4):
                if h % 2 == 0:
                    act(ku[:, h, :], pv[:, h * 128:(h + 1) * 128], AF.Copy, [pb.b, GWs[cg[0] % 2].b], [ku.b], scale=GWs[cg[0] % 2][:, j, 12 + h:13 + h])
                else:
                    ts(ku[:, h, :], pv[:, h * 128:(h + 1) * 128], GWs[cg[0] % 2][:, j, 12 + h:13 + h], None, ALU.mult, None,
                       [pb.b, GWs[cg[0] % 2].b], [ku.b])
            nbanks = []
            if full:
                for hp in range(2):
                    pn = bank("at")
                    nbanks.append(pn)
                    for hh in range(2):
                        h = hp * 2 + hh
                        ps_ = bank("fm")
                        mm([(ps_[:, 0:128], QKMs[cg[0] % 2][:, 4 + h, blk], QKMs[cg[0] % 2][:, h, blk], True, True)], [QKMs[cg[0] % 2].b], [ps_.b])
                        at = AT[0]
                        stt(at[:], ps_[:, 0:128], GWs[cg[0] % 2][:, j, 8 + h:9 + h], maskC[:], ALU.mult, ALU.mult,
                            [ps_.b, GWs[cg[0] % 2].b, maskC.b], [at.b])
                        mm([(pn[:, hh * 129:(hh + 1) * 129], at[:], vm[:, h, 0:129], True, False),
                            (pn[:, hh * 129:(hh + 1) * 129], QKMs[cg[0] % 2][:, h, blk], CB[:, h, 0:129], False, True)],
                           [at.b, vm.b, QKMs[cg[0] % 2].b, CB.b], [pn.b])
            for h in range(4):
                pd = bank("tm")
                mm([(pd[:, 0:129], ku[:, h, :], vm[:, h, 0:129], True, True)], [ku.b, vm.b], [pd.b])
                stt(DST[:, h, :], DST[:, h, :], glp[:, h:h + 1], pd[:, 0:129], ALU.mult, ALU.add,
                    [DST.b, glp.b, pd.b], [DST.b])
            return nbanks

        def state_to_cb(par):
            glc = GL[par]
            for h in range(4):
                act(CB[:, h, 0:129], DST[:, h, :], AF.Copy, [DST.b, glc.b], [CB.b], scale=glc[:, 4 + h:5 + h])

        def mlstm_out(j, nbanks):
            blk = slice(j * 128, (j + 1) * 128)
            to = TO[cg[0] % 2]
            for hp in range(2):
                pn = nbanks[hp]
                for hh in range(2):
                    h = hp * 2 + hh
                    cp("dve", DEN[:, h:h + 1], pn[:, hh * 129 + 128:hh * 129 + 129], [pn.b], [DEN.b])
                    S.op("dve", lambda e, h=h, hh=hh, pn=pn: e.reduce_sum(out=DEN[:, 4 + h:5 + h], in_=pn[:, hh * 129:hh * 129 + 128], axis=AX.X),
                         [pn.b], [DEN.b])
            tt(DEN[:, 0:4], DEN[:, 0:4], GWs[cg[0] % 2][:, j, 16:20], ALU.mult, [DEN.b, GWs[cg[0] % 2].b], [DEN.b])
            ts(DEN[:, 8:12], DEN[:, 0:4], -1.0, None, ALU.mult, None, [DEN.b], [DEN.b])
            tt(DEN[:, 0:4], DEN[:, 0:4], DEN[:, 8:12], ALU.max, [DEN.b], [DEN.b])
            ts(DEN[:, 0:4], DEN[:, 0:4], 1.0, None, ALU.max, None, [DEN.b], [DEN.b])
            S.op("dve", lambda e: e.reciprocal(out=DEN[:, 0:4], in_=DEN[:, 0:4]), [DEN.b], [DEN.b])
            tt(DEN[:, 0:4], DEN[:, 0:4], GWs[cg[0] % 2][:, j, 16:20], ALU.mult, [DEN.b, GWs[cg[0] % 2].b], [DEN.b])
            tt(DEN[:, 8:12], DEN[:, 0:4], DEN[:, 0:4], ALU.mult, [DEN.b], [DEN.b])
            S.op("dve", lambda e: e.reciprocal(out=DEN[:, 8:12], in_=DEN[:, 8:12]), [DEN.b], [DEN.b])
            ts(DEN[:, 4:8], DEN[:, 4:8], -1.0 / 128, None, ALU.mult, None, [DEN.b], [DEN.b])
            for hp in range(2):
                pn = nbanks[hp]
                for hh in range(2):
                    h = hp * 2 + hh
                    ts(CEN[:, h, :], pn[:, hh * 129:hh * 129 + 128], DEN[:, 4 + h:5 + h], None, ALU.add, None,
                       [pn.b, DEN.b], [CEN.b])
            act(SQ[:, 0:512].rearrange("p (h d) -> p h d", h=4), CEN[:], AF.Square, [CEN.b], [SQ.b])
            S.op("dve", lambda e: e.reduce_sum(out=DEN[:, 12:16], in_=SQ[:, 0:512].rearrange("p (h d) -> p h d", h=4), axis=AX.X), [SQ.b], [DEN.b])
            ts(DEN[:, 12:16], DEN[:, 12:16], 1.0 / 128, None, ALU.mult, None, [DEN.b], [DEN.b])
            stt(DEN[:, 12:16], DEN[:, 8:12], EPS, DEN[:, 12:16], ALU.mult, ALU.add, [DEN.b], [DEN.b])
            act(DEN[:, 12:16], DEN[:, 12:16], AF.Sqrt, [DEN.b], [DEN.b])
            S.op("dve", lambda e: e.reciprocal(out=DEN[:, 12:16], in_=DEN[:, 12:16]), [DEN.b], [DEN.b])
            for h in range(4):
                act(TMPX[:, h * 128:(h + 1) * 128], CEN[:, h, :], AF.Copy, [CEN.b, DEN.b], [TMPX.b], scale=DEN[:, 12 + h:13 + h])
            stt(TMPX[:, 0:512], to[:], 1.0, TMPX[:, 0:512], ALU.add, ALU.mult, [to.b, TMPX.b], [TMPX.b])
            tt(YM[:], TMPX[:, 0:512], nwh[:], ALU.mult, [TMPX.b, nwh.b], [YM.b])
            pb = bank("tr")
            pv = pb[:, :].bitcast(BF16)
            transposes([(pv[:, c * 128:(c + 1) * 128], YM[:, c * 128:(c + 1) * 128], ident[:]) for c in range(4)],
                       [YM.b, ident.b], [pb.b])
            cp("act", YMT[:, :, blk], pv[:, 0:512].rearrange("p (c t) -> p c t", c=4), [pb.b], [YMT.b])

        def attention_block(j, kt):
            blk = slice(j * 128, (j + 1) * 128)
            va_c = VA[cg[0] % 3]
            va_p = VA[(cg[0] - 1) % 3]
            for g in range(2):
                rows_ = slice(g * 64, (g + 1) * 64)
                pts = []
                for which, kcols, mask in (("p", slice(j * 128, j * 128 + 128), maskP), ("c", slice(128 + j * 128, 256 + j * 128), maskC)):
                    pb = bank("at")
                    mm([(pb[:, :], kt[rows_, kcols], QTs[cg[0] % 2][rows_, :, blk], True, True)], [kt.b, QTs[cg[0] % 2].b], [pb.b])
                    pt = PT[len(pts)]
                    act(pt[:], pb[:, :], AF.Exp, [pb.b], [pt.b], scale=0.125)
                    ptv = pt[:, :].rearrange("p (h t) -> p h t", h=4)
                    tt(ptv, ptv, mask[:, :].unsqueeze(1).to_broadcast([128, 4, 128]), ALU.mult, [pt.b, mask.b], [pt.b])
                    pts.append(pt)
                po = bank("tm")
                items = []
                for h in range(4):
                    o = po[:, h * 65:(h + 1) * 65]
                    items.append((o, pts[0][:, h * 128:(h + 1) * 128], va_p[:, g, 0:65], True, False))
                    items.append((o, pts[1][:, h * 128:(h + 1) * 128], va_c[:, g, 0:65], False, True))
                mm(items, [pts[0].b, pts[1].b, va_p.b, va_c.b], [po.b])
                pov = po[:, 0:260].rearrange("p (h d) -> p h d", h=4)
                tt(DEN[:, 0:4], pov[:, :, 64], esink[:, g * 4:(g + 1) * 4], ALU.add, [po.b, esink.b], [DEN.b])
                S.op("dve", lambda e: e.reciprocal(out=DEN[:, 0:4], in_=DEN[:, 0:4]), [DEN.b], [DEN.b])
                yav = YA[:, g * 256:(g + 1) * 256].rearrange("p (h d) -> p h d", h=4)
                tt(yav, pov[:, :, 0:64], DEN[:, 0:4].unsqueeze(2).to_broadcast([128, 4, 64]), ALU.mult, [po.b, DEN.b], [YA.b])
            pb = bank("tr")
            pv = pb[:, :].bitcast(BF16)
            transposes([(pv[:, c * 128:(c + 1) * 128], YA[:, c * 128:(c + 1) * 128], ident[:]) for c in range(4)],
                       [YA.b, ident.b], [pb.b])
            cp("dve", YAT[:, :, blk], pv[:, 0:512].rearrange("p (c t) -> p c t", c=4), [pb.b], [YAT.b])

        def h1(g, is_pre, is_last_pre, xsrc, pos_ap, ti):
            full = not is_pre
            xt = XT[g % 2]
            a, shoff = amod["m"]
            stage_A(xsrc, xt, a, shoff, HTs[g % 2], NB, SQ, XS, stat)
            yield
            kt = KT[g % 2]
            ktp = KT[1 - g % 2]
            ro = 0
            if is_last_pre:
                rope_tables(pos_ap, T)
            elif full:
                ro = (ti * T) % RTW
                if ro == 0:
                    w = min(RTW, NTOK - ti * T)
                    rope_tables(posm[:, ti * T:ti * T + w], w)
            yield
            if full or is_last_pre:
                cp("dve", kt[:, 0:128], ktp[:, T:T + 128], [ktp.b], [kt.b])
                rope_group(C_KA, kt[:, 128:128 + T], kt.b, 0, ro)
                yield
            if full:
                for qj in range(4):
                    rope_group(C_QA + qj * 128, QTs[g % 2][:, qj, :], QTs[g % 2].b, qj + 1, ro)
                    yield
            for gi in range(8):
                if gi >= 4 or full or is_last_pre:
                    conv_group(gi, (C_QM if gi < 4 else C_KM) + (gi % 4) * 128)
                    yield
            tm_proj(0, full)
            yield

        def gates(g, lo, hi):
            for q4 in range(lo // 4, hi // 4):
                pb = bank("tm")
                mm([(pb[:, :], HTs[g % 2][:, kc, :], win(kc, C_GA + q4 * 512, C_GA + (q4 + 1) * 512), kc == 0, kc == KC - 1) for kc in range(KC)],
                   [WIN.b, HTs[g % 2].b], [pb.b])
                act(TG[:, q4 * 512:(q4 + 1) * 512], pb[:, :], AF.Tanh, [pb.b], [TG.b], scale=0.5)

        def h2(g, is_pre, is_last_pre, ti):
            full = not is_pre
            nb = NB
            xt = XT[g % 2]
            kt = KT[g % 2]
            gpb = gate_math(nb)
            yield
            if full:
                gates(g, 0, 4)
                yield
            nbanks = mlstm_block(0, gpb, full)
            if is_last_pre:
                ts(DST[:], DST[:], flag[:, 0:1], None, ALU.mult, None, [DST.b, flag.b], [DST.b])
                va = VA[g % 3]
                ts(va[:], va[:], flag[:, 0:1], None, ALU.mult, None, [va.b, flag.b], [va.b])
                ts(UU[:, :, T:T + 4], UU[:, :, T:T + 4], flag[:, 0:1], None, ALU.mult, None, [UU.b, flag.b], [UU.b])
            yield
            if full:
                gates(g, 4, 8)
                yield
                mlstm_out(0, nbanks)
                yield
                gates(g, 8, 12)
                yield
                attention_block(0, kt)
                yield
                gates(g, 12, 16)
                yield
            state_to_cb(g % 2)
            yield
            if full:
                for c in range(2):
                    pa = bank("tm")
                    mm([(pa[:, :], YAT[:, kc, :], WA[:, kc * D + c * 512:kc * D + (c + 1) * 512], kc == 0, kc == 3) for kc in range(4)],
                       [WA.b, YAT.b], [pa.b])
                    pm_ = bank("at")
                    mm([(pm_[:, :], YMT[:, kc, :], WB[:, kc * D + c * 512:kc * D + (c + 1) * 512], kc == 0, kc == 3) for kc in range(4)],
                       [WB.b, YMT.b], [pm_.b])
                    stt(M1[:], TG[:, c * 512:(c + 1) * 512], 1.0, pa[:, :], ALU.add, ALU.mult, [TG.b, pa.b], [M1.b])
                    stt(M2[:], TG[:, D + c * 512:D + (c + 1) * 512], 1.0, pm_[:, :], ALU.add, ALU.mult, [TG.b, pm_.b], [M2.b])
                    tt(MTOK[:, c * 512:(c + 1) * 512], M1[:], M2[:], ALU.add, [M1.b, M2.b], [MTOK.b])
                    yield
                pb = bank("tr")
                pv = pb[:, :].bitcast(BF16)
                transposes([(pv[:, kc * 128:(kc + 1) * 128], MTOK[:, kc * 128:(kc + 1) * 128], ident[:]) for kc in range(KC)],
                           [MTOK.b, ident.b], [pb.b])
                cp("act", MT[:, 0:4, :], pv[:, 0:512].rearrange("p (c t) -> p c t", c=4), [pb.b], [MT.b])
                cp("dve", MT[:, 4:8, :], pv[:, 512:1024].rearrange("p (c t) -> p c t", c=4), [pb.b], [MT.b])
                yield
                pair = []
                for c in range(2):
                    pb = bank("tm" if c == 0 else "at")
                    mm([(pb[:, :], MT[:, kc, :], WO[:, kc * D + c * 512:kc * D + (c + 1) * 512], kc == 0, kc == KC - 1) for kc in range(KC)],
                       [MT.b, WO.b], [pb.b])
                    act(SQ[:, c * 512:(c + 1) * 512], pb[:, :], AF.Square, [pb.b], [SQ.b])
                    pair.append(pb)
                S.op("dve", lambda e: e.reduce_sum(out=stat[:, 4:5], in_=SQ[:], axis=AX.X), [SQ.b], [stat.b])
                yield
                ts(stat[:, 4:5], stat[:, 4:5], 1.0 / D, 4 * EPS, ALU.mult, ALU.add, [stat.b], [stat.b])
                act(stat[:, 4:5], stat[:, 4:5], AF.Sqrt, [stat.b], [stat.b])
                S.op("dve", lambda e: e.reciprocal(out=stat[:, 4:5], in_=stat[:, 4:5]), [stat.b], [stat.b])
                gb = ti
                for c in range(2):
                    act(TMPX[:, c * 512:(c + 1) * 512], pair[c][:, :], AF.Copy, [pair[c].b, stat.b], [TMPX.b], scale=stat[:, 4:5])
                tt(TMPX[:], TMPX[:], gg["m"][:], ALU.mult, [TMPX.b, gg["m"].b], [TMPX.b])
                tt(xt[:, 0, :], TMPX[:], xt[:, 0, :], ALU.add, [TMPX.b, xt.b], [xt.b])
                S.dma("sp", out_d[gb * 128:(gb + 1) * 128, :], xt[:, 0, :], reads=[xt.b], writes=[OD[gb]], sem=dsem_of(xt))
                yield

        assert NB == 1
        tiles = []
        for ti in range(NT_PRE):
            tiles.append((True, ti == NT_PRE - 1, xp[ti * T:(ti + 1) * T, :].rearrange("(b p) f -> p b f", p=128), posp, ti))
        for ti in range(NT_MAIN):
            tiles.append((False, False, xm[ti * T:(ti + 1) * T, :].rearrange("(b p) f -> p b f", p=128), None, ti))

        def run_gen(gen, g):
            cg[0] = g
            try:
                next(gen)
                return True
            except StopIteration:
                return False

        def mk1(g):
            is_pre, last, xsrc, pos_ap, ti = tiles[g]
            return h1(g, is_pre, last, xsrc, pos_ap, ti)

        def mk2(g):
            is_pre, last, xsrc, pos_ap, ti = tiles[g]
            return h2(g, is_pre, last, ti)

        g1 = mk1(0)
        while run_gen(g1, 0):
            pass
        for g in range(len(tiles)):
            a2 = mk2(g)
            b1 = mk1(g + 1) if g + 1 < len(tiles) else None
            alive2, alive1 = True, b1 is not None
            while alive2 or alive1:
                if alive2:
                    alive2 = run_gen(a2, g)
                if alive1:
                    alive1 = run_gen(b1, g + 1)
        stop('p1')
        S.barrier()
        p1.close()
        pw.close()
        p2 = es.enter_context(ExitStack())
        WG = sb([128, KC * DFF], BF16, "WG", p2)
        WU = sb([128, KC * DFF], BF16, "WU", p2)
        WD = sb([128, NSLAB * D], BF16, "WD", p2)
        XT2 = [sb([128, NB2, D], F32, "XT2_%d" % i, p2) for i in range(1)]
        SQ2 = sb([128, D], BF16, "SQ2", p2)
        XS2 = [sb([128, D], BF16, "XS2_%d" % i, p2) for i in range(2)]
        H2 = [sb([128, KC, T2], BF16, "H2_%d" % i, p2) for i in range(1)]
        stat2 = sb([128, 8], F32, "stat2", p2)
        TGF = [sb([128, T2], F32, "TGF%d" % i, p2) for i in range(2)]
        SGF = [sb([128, T2], F32, "SGF%d" % i, p2) for i in range(2)]
        ACTT = [sb([128, T2], BF16, "ACTT%d" % i, p2) for i in range(2)]
        TMPX2 = sb([128, D], F32, "TMPX2", p2)
        OUTB = [sb([128, D], F32, "OUTB%d" % i, p2) for i in range(1)]
        OD2 = [Buf("od2_%d" % i) for i in range(NTOK // 128)]
        load_cast(WG[:], wg_d, KC * DFF, WG.b)
        load_cast(WU[:], wu_d, KC * DFF, WU.b)
        load_cast(WD[:], wd_d, NSLAB * D, WD.b)
        rr["ff"] = [0, [4, 5, 6, 7]]
        rr["gu"] = [0, [2, 3]]
        a, shoff = amod["f"]
        for ti in range(NTOK // T2):
            xt = XT2[0]
            h2 = H2[0]
            stage_A(out_d[ti * T2:(ti + 1) * T2, :].rearrange("(b p) f -> p b f", p=128), xt, a, shoff, h2, NB2, SQ2, XS2, stat2,
                    xdeps=[OD[ti * NB2 + j] for j in range(NB2)])
            ffb = [[banks[4 + j * 2 + c] for c in range(2)] for j in range(NB2)]
            def gu_stage(s):
                pg = bank("gu")
                mm([(pg[:, 0:T2], WG[:, kc * DFF + s * 128:kc * DFF + (s + 1) * 128], h2[:, kc, :], kc == 0, kc == KC - 1) for kc in range(KC)]
                   + [(pg[:, T2:2 * T2], WU[:, kc * DFF + s * 128:kc * DFF + (s + 1) * 128], h2[:, kc, :], kc == 0, kc == KC - 1) for kc in range(KC)],
                   [WG.b, WU.b, h2.b], [pg.b])
                tg, sg, at = TGF[s % 2], SGF[s % 2], ACTT[s % 2]
                act(tg[:], pg[:, 0:T2], AF.Tanh, [pg.b], [tg.b], scale=0.5)
                stt(sg[:], tg[:], 1.0, pg[:, 0:T2], ALU.add, ALU.mult, [tg.b, pg.b], [sg.b])
                tt(at[:], sg[:], pg[:, T2:2 * T2], ALU.mult, [sg.b, pg.b], [at.b])

            def down_stage(s):
                at = ACTT[s % 2]
                items = []
                for j in range(NB2):
                    for c in range(2):
                        items.append((ffb[j][c][:, :], at[:, j * 128:(j + 1) * 128], WD[:, s * D + c * 512:s * D + (c + 1) * 512],
                                      s == 0, s == NSLAB - 1))
                mm(items, [at.b, WD.b], [ffb[j][c].b for j in range(NB2) for c in range(2)])

            for s in range(NSLAB + 1):
                if s < NSLAB:
                    gu_stage(s)
                if s >= 1:
                    down_stage(s - 1)
            for j in range(NB2):
                for c in range(2):
                    act(SQ2[:, c * 512:(c + 1) * 512], ffb[j][c][:, :], AF.Square, [ffb[j][c].b], [SQ2.b])
                S.op("dve", lambda e, j=j: e.reduce_sum(out=stat2[:, 4 + j:5 + j], in_=SQ2[:], axis=AX.X), [SQ2.b], [stat2.b])
            ts(stat2[:, 4:4 + NB2], stat2[:, 4:4 + NB2], 1.0 / D, 4 * EPS, ALU.mult, ALU.add, [stat2.b], [stat2.b])
            act(stat2[:, 4:4 + NB2], stat2[:, 4:4 + NB2], AF.Sqrt, [stat2.b], [stat2.b])
            S.op("dve", lambda e: e.reciprocal(out=stat2[:, 4:4 + NB2], in_=stat2[:, 4:4 + NB2]), [stat2.b], [stat2.b])
            for j in range(NB2):
                gb = ti * NB2 + j
                ob = OUTB[0]
                for c in range(2):
                    act(TMPX2[:, c * 512:(c + 1) * 512], ffb[j][c][:, :], AF.Copy, [ffb[j][c].b, stat2.b], [TMPX2.b],
                        scale=stat2[:, 4 + j:5 + j])
                tt(TMPX2[:], TMPX2[:], gg["f"][:], ALU.mult, [TMPX2.b, gg["f"].b], [TMPX2.b])
                tt(ob[:], TMPX2[:], xt[:, j, :], ALU.add, [TMPX2.b, xt.b], [ob.b])
                S.dma("sp", out_d[gb * 128:(gb + 1) * 128, :], ob[:], reads=[ob.b, OD[gb]], writes=[OD2[gb]], sem=dsem_of(ob))
        waits = S._deps("sp", OD2, ())

        def fin(e):
            for s, v in waits:
                e.wait_ge(s, v)

        S.q["sp"].append(fin)
        S.emit()
        p2.close()
    except _Stop:
        pass
    return nc


def _consts():
    ident = np.eye(128, dtype=np.float32)
    s = np.arange(128)[:, None]
    t = np.arange(128)[None, :]
    maskC = (s <= t).astype(np.float32)
    maskP = (s > t).astype(np.float32)
    rot = np.zeros((128, 128), np.float32)
    for m in range(128):
        d = m % 64
        base = m - d
        if d < 32:
            rot[base + d + 32, m] = -1.0
        else:
            rot[base + d - 32, m] = 1.0
    half = 32
    inv = (10000.0 ** (-2.0 * np.arange(half, dtype=np.float32) / 64)).astype(np.float32)
    invf = np.tile(inv, 4)[:, None].astype(np.float32)
    return np.concatenate([ident, maskC, maskP, rot, invf], axis=1)


def _kc_layout(w):
    K, N = w.shape
    return np.ascontiguousarray(w.reshape(K // 128, 128, N).transpose(1, 0, 2).reshape(128, (K // 128) * N))


def _prep_weights(w_ada, b_ada, g_pre_mix, g_post_mix, w_in, b_if, conv_w, conv_b, attn_sinks, mlstm_norm_w,
                  w_branch_attn, w_branch_mlstm, w_out, g_pre_ffn, g_post_ffn, w_ffn_gate, w_ffn_up, w_ffn_down):
    l = 0
    qperm = np.concatenate([np.arange(64 * h, 64 * h + 64) for h in (0, 4, 1, 5, 2, 6, 3, 7)])
    cols = np.concatenate([qperm, np.arange(512, 640), np.arange(768, 1280), np.arange(1280, 1792),
                           np.arange(2824, 3848), np.arange(3848, 4872), np.arange(1792, 2304),
                           np.arange(2304, 2816), np.arange(640, 768), np.arange(2816, 2824)])
    wa = w_ada[l].reshape(KC, 128, 12, 512).transpose(1, 2, 0, 3).reshape(128, 12, KC * 512)
    fm = lambda v: np.ascontiguousarray(v.reshape(KC, 128).T)
    return {
        "w_ada": np.ascontiguousarray(wa), "b_ada": np.ascontiguousarray(b_ada[l][None, :]),
        "gpm": fm(g_pre_mix[l]), "gpf": fm(g_pre_ffn[l]),
        "gpostm": np.ascontiguousarray(g_post_mix[l][None, :]), "gpostf": np.ascontiguousarray(g_post_ffn[l][None, :]),
        "w_in": _kc_layout(w_in[l][:, cols]), "bif": np.ascontiguousarray(b_if[l][None, :]),
        "convw": np.ascontiguousarray(conv_w[l].reshape(4, 8, 128).transpose(2, 1, 0).reshape(128, 32)),
        "convb": np.ascontiguousarray(conv_b[l].reshape(8, 128).T),
        "sinks": np.ascontiguousarray(attn_sinks[l][None, :]), "nw": np.ascontiguousarray(mlstm_norm_w[l][None, :]),
        "wa": _kc_layout(w_branch_attn[l]), "wb": _kc_layout(w_branch_mlstm[l]), "wo": _kc_layout(w_out[l]),
        "wg": _kc_layout(w_ffn_gate[l]), "wu": _kc_layout(w_ffn_up[l]), "wd": _kc_layout(w_ffn_down[l]),
        "cst": _consts(),
    }


def make_in_maps(x, c, positions, wts, ntok):
    B, S_, _ = x.shape
    per = S_ // ntok
    maps = []
    for b in range(B):
        for hseg in range(per):
            lo = hseg * ntok
            m = dict(wts)
            m["xm"] = np.ascontiguousarray(x[b, lo:lo + ntok])
            if hseg == 0:
                m["xp"] = np.zeros((ntok, D), np.float32)
                m["posp"] = np.ascontiguousarray(positions[b:b + 1, 0:T])
                m["flag"] = np.zeros((128, 1), np.float32)
            else:
                m["xp"] = np.ascontiguousarray(x[b, lo - ntok:lo])
                m["posp"] = np.ascontiguousarray(positions[b:b + 1, lo - T:lo])
                m["flag"] = np.ones((128, 1), np.float32)
            m["posm"] = np.ascontiguousarray(positions[b:b + 1, lo:lo + ntok])
            m["cT"] = np.ascontiguousarray(c[b].reshape(KC, 128).T)
            maps.append(m)
    return maps


def kernel(x, c, positions, w_ada, b_ada, g_pre_mix, g_post_mix, w_in, b_if, conv_w, conv_b, attn_sinks,
           mlstm_norm_w, w_branch_attn, w_branch_mlstm, w_out, g_pre_ffn, g_post_ffn, w_ffn_gate, w_ffn_up,
           w_ffn_down):
    x = np.asarray(x, np.float32)
    B, S_, _ = x.shape
    ntok = S_ // 2
    wts = _prep_weights(*[np.asarray(a, np.float32) for a in (
        w_ada, b_ada, g_pre_mix, g_post_mix, w_in, b_if, conv_w, conv_b, attn_sinks, mlstm_norm_w,
        w_branch_attn, w_branch_mlstm, w_out, g_pre_ffn, g_post_ffn, w_ffn_gate, w_ffn_up, w_ffn_down)])
    maps = make_in_maps(x, np.asarray(c, np.float32), np.asarray(positions, np.int32), wts, ntok)
    nc = build(ntok, ntok)
    res = run_bass_kernel_spmd(nc, maps, core_ids=list(range(len(maps))))
    out = np.empty((B, S_, D), np.float32)
    i = 0
    for b in range(B):
        for hseg in range(S_ // ntok):
            out[b, hseg * ntok:(hseg + 1) * ntok] = res.results[i]["out"]
            i += 1
    return out
```
